# Optimizing a Trainium2 kernel written in Bass

```python
import math
import jax, jax.numpy as jnp
from jax import lax
import numpy as np

D_MODEL = 2048
BATCH = 8
SEQ = 2048
DEPTH = 4
DEC_BATCH = 4
DEC_SEQ = 4096
PAST_LEN = 128

GRID_W = 64
N_MIXERS = 2
N_ATTN_LAYERS = (DEPTH + 1) // 2
N_MLSTM_LAYERS = DEPTH // 2
HEAD_DIM = 128
N_Q_HEADS = D_MODEL // HEAD_DIM
N_KV_HEADS = N_Q_HEADS // 4
Q_BLOCK = 128
ROPE_THETA = 10000.0
ATTN_IN = (N_Q_HEADS + 2 * N_KV_HEADS) * HEAD_DIM
ML_HEADS = 8
ML_DV = D_MODEL // ML_HEADS
ML_DQK = ML_DV // 2
ML_CHUNK = 64
ML_GATES = 4 * ML_HEADS
ML_IN = 2 * ML_HEADS * ML_DQK + 2 * D_MODEL + ML_GATES
MEM_LEN = 256
XA_HEADS = 4
XA_HEAD_DIM = D_MODEL // XA_HEADS
D_FF = 4 * D_MODEL
DN_ALPHA = (2 * DEPTH) ** 0.25
DN_BETA = (8 * DEPTH) ** -0.25
LN_EPS = 1e-5
RMS_EPS = 1e-6

kernel_name = "hybrid_gqa_mlstm_deepnorm_encoder"

F32 = jnp.float32


def _layer_norm(x, g, b):
    xf = x.astype(F32)
    mu = xf.mean(-1, keepdims=True)
    var = jnp.square(xf - mu).mean(-1, keepdims=True)
    return ((xf - mu) * lax.rsqrt(var + LN_EPS) * g.astype(F32) + b.astype(F32)).astype(x.dtype)


def _rms_norm(x, g):
    xf = x.astype(F32)
    return (xf * lax.rsqrt(jnp.mean(xf * xf, -1, keepdims=True) + RMS_EPS) * g.astype(F32)).astype(x.dtype)


def _axial_rope_tables(S):
    rows = S // GRID_W
    row_ids = jnp.repeat(jnp.arange(rows), GRID_W).astype(F32)
    col_ids = jnp.tile(jnp.arange(GRID_W), rows).astype(F32)
    axis_dim = HEAD_DIM // 2
    inv_freq = ROPE_THETA ** (-jnp.arange(0, axis_dim, 2, dtype=F32) / axis_dim)
    ang = jnp.stack([row_ids[:, None] * inv_freq, col_ids[:, None] * inv_freq], axis=1)
    return jnp.cos(ang), jnp.sin(ang)


def _apply_axial_rope(x, cos, sin):
    B, S, H, _ = x.shape
    xs = x.astype(F32).reshape(B, S, H, 2, 2, HEAD_DIM // 4)
    x1, x2 = xs[..., 0, :], xs[..., 1, :]
    c, s = cos[:, None], sin[:, None]
    out = jnp.stack([x1 * c - x2 * s, x1 * s + x2 * c], axis=-2)
    return out.reshape(B, S, H, HEAD_DIM).astype(x.dtype)


def _gqa_axial(x, w_in, q_gain, k_gain, w_out):
    B, S, _ = x.shape
    h = x @ w_in
    q, k, v = jnp.split(h, [N_Q_HEADS * HEAD_DIM, (N_Q_HEADS + N_KV_HEADS) * HEAD_DIM], axis=-1)
    q = q.reshape(B, S, N_Q_HEADS, HEAD_DIM)
    k = k.reshape(B, S, N_KV_HEADS, HEAD_DIM)
    v = v.reshape(B, S, N_KV_HEADS, HEAD_DIM)
    cos, sin = _axial_rope_tables(S)
    q = _apply_axial_rope(_rms_norm(q, q_gain), cos, sin)
    k = _apply_axial_rope(_rms_norm(k, k_gain), cos, sin)
    G = N_Q_HEADS // N_KV_HEADS
    q = q.transpose(0, 2, 1, 3).reshape(B, N_KV_HEADS, G, S, HEAD_DIM)
    k = k.transpose(0, 2, 1, 3)
    v = v.transpose(0, 2, 1, 3)
    nb = S // Q_BLOCK
    qb = jnp.moveaxis(q.reshape(B, N_KV_HEADS, G, nb, Q_BLOCK, HEAD_DIM), 3, 0)
    scale = HEAD_DIM ** -0.5

    def block(qblk):
        s = jnp.einsum('bkgqd,bksd->bkgqs', qblk, k, preferred_element_type=F32) * scale
        p = jax.nn.softmax(s, axis=-1).astype(v.dtype)
        return jnp.einsum('bkgqs,bksd->bkgqd', p, v)

    o = lax.map(block, qb)
    o = jnp.moveaxis(o, 0, 3).reshape(B, N_Q_HEADS, S, HEAD_DIM)
    o = o.transpose(0, 2, 1, 3).reshape(B, S, N_Q_HEADS * HEAD_DIM)
    return o @ w_out


def _mlstm_chunkwise(q, k, v, log_i, log_f):
    B, H, S, DK = q.shape
    DV = v.shape[-1]
    nc = S // ML_CHUNK

    def chunks(a):
        a = a.reshape(B, H, nc, ML_CHUNK, *a.shape[3:])
        return jnp.moveaxis(a, 2, 0)

    xs = tuple(chunks(a) for a in (q, k, v, log_i, log_f))
    lower = jnp.tril(jnp.ones((ML_CHUNK, ML_CHUNK), dtype=bool))

    def step(carry, inp):
        C, n, m = carry
        qj, kj, vj, ij, fj = inp
        b = jnp.cumsum(fj, axis=-1)
        d = b[..., :, None] - b[..., None, :] + ij[..., None, :]
        d = jnp.where(lower, d, -jnp.inf)
        inter = b + m[..., None]
        m_j = jnp.maximum(inter, d.max(-1))
        w = jnp.exp(d - m_j[..., None])
        g = jnp.exp(inter - m_j)
        s = jnp.einsum('bhld,bhsd->bhls', qj, kj) * w
        num = g[..., None] * jnp.einsum('bhvd,bhld->bhlv', C, qj) + jnp.einsum('bhls,bhsv->bhlv', s, vj)
        den = g * jnp.einsum('bhd,bhld->bhl', n, qj) + s.sum(-1)
        h = num / jnp.maximum(jnp.abs(den), jnp.exp(-m_j))[..., None]
        bL = b[..., -1]
        dl = bL[..., None] - b + ij
        m_new = jnp.maximum(bL + m, dl.max(-1))
        gs = jnp.exp(bL + m - m_new)
        ws = jnp.exp(dl - m_new[..., None])
        C = gs[..., None, None] * C + jnp.einsum('bhs,bhsv,bhsd->bhvd', ws, vj, kj)
        n = gs[..., None] * n + jnp.einsum('bhs,bhsd->bhd', ws, kj)
        return (C, n, m_new), h

    init = (jnp.zeros((B, H, DV, DK), F32), jnp.zeros((B, H, DK), F32), jnp.zeros((B, H), F32))
    _, hs = lax.scan(step, init, xs)
    return jnp.moveaxis(hs, 0, 2).reshape(B, H, S, DV)


def _mlstm_bidir(x, w_in, b_gate, head_gain, w_out):
    B, S, _ = x.shape
    h = x @ w_in
    nqk = ML_HEADS * ML_DQK
    q, k, v, o, gates = jnp.split(h, [nqk, 2 * nqk, 2 * nqk + D_MODEL, 2 * nqk + 2 * D_MODEL], axis=-1)

    def to_heads(a, d):
        return a.reshape(B, S, ML_HEADS, d).transpose(0, 2, 1, 3).astype(F32)

    q = to_heads(q, ML_DQK)
    k = to_heads(k, ML_DQK) * (ML_DQK ** -0.5)
    v = to_heads(v, ML_DV)
    gates = (gates.reshape(B, S, 4, ML_HEADS) + b_gate).astype(F32).transpose(2, 0, 3, 1)
    log_i_f, log_f_f = gates[0], jax.nn.log_sigmoid(gates[1])
    log_i_b, log_f_b = gates[2], jax.nn.log_sigmoid(gates[3])
    h_f = _mlstm_chunkwise(q, k, v, log_i_f, log_f_f)

    def flip(a):
        return jnp.flip(a, axis=2)

    h_b = flip(_mlstm_chunkwise(flip(q), flip(k), flip(v), flip(log_i_b), flip(log_f_b)))
    hs = (h_f + h_b).transpose(0, 2, 1, 3)
    hs = _rms_norm(hs, head_gain).reshape(B, S, D_MODEL)
    hs = hs * jax.nn.sigmoid(o.astype(F32))
    return hs.astype(x.dtype) @ w_out


def _mem_cross_attn(x, mem, w_q, w_kv, w_out):
    B, S, _ = x.shape
    M = mem.shape[1]
    q = (x @ w_q).reshape(B, S, XA_HEADS, XA_HEAD_DIM)
    kv = (mem @ w_kv).reshape(B, M, 2, XA_HEADS, XA_HEAD_DIM)
    s = jnp.einsum('bshd,bmhd->bhsm', q, kv[:, :, 0], preferred_element_type=F32) * (XA_HEAD_DIM ** -0.5)
    p = jax.nn.softmax(s, axis=-1).astype(x.dtype)
    o = jnp.einsum('bhsm,bmhd->bshd', p, kv[:, :, 1]).reshape(B, S, D_MODEL)
    return o @ w_out


def _sq_relu_mlp(x, w1, w2):
    return jnp.square(jax.nn.relu(x @ w1)) @ w2


def _trunk(x, mem, p):
    for i in range(DEPTH):
        j = i // N_MIXERS
        if i % N_MIXERS == 0:
            y = _gqa_axial(x, p['attn_w_in'][j], p['attn_q_gain'][j], p['attn_k_gain'][j], p['attn_w_out'][j])
        else:
            y = _mlstm_bidir(x, p['ml_w_in'][j], p['ml_b_gate'][j], p['ml_head_gain'][j], p['ml_w_out'][j])
        x = _layer_norm(DN_ALPHA * x + y, p['ln_g'][i, 0], p['ln_b'][i, 0])
        y = _mem_cross_attn(x, mem, p['xa_w_q'][i], p['xa_w_kv'][i], p['xa_w_out'][i])
        x = _layer_norm(DN_ALPHA * x + y, p['ln_g'][i, 1], p['ln_b'][i, 1])
        y = _sq_relu_mlp(x, p['mlp_w1'][i], p['mlp_w2'][i])
        x = _layer_norm(DN_ALPHA * x + y, p['ln_g'][i, 2], p['ln_b'][i, 2])
    return x


def _normal(key, shape, scale):
    return jax.random.normal(key, shape, F32) * scale


def setup_inputs(seed: int = 0) -> dict:
    key = jax.random.key(seed)
    ks = jax.random.split(key, 19)
    D = D_MODEL
    gate_base = jnp.array([0.0, 3.0, 0.0, 3.0], F32)[None, :, None]
    return {
        'x_prompt': _normal(ks[0], (BATCH, SEQ, D), 1.0),
        'x_sample': _normal(ks[1], (DEC_BATCH, DEC_SEQ, D), 1.0),
        'mem_prompt': _normal(ks[2], (BATCH, MEM_LEN, D), 1.0),
        'mem_sample': _normal(ks[3], (DEC_BATCH, MEM_LEN, D), 1.0),
        'attn_w_in': _normal(ks[4], (N_ATTN_LAYERS, D, ATTN_IN), D ** -0.5),
        'attn_q_gain': 1.0 + _normal(ks[5], (N_ATTN_LAYERS, HEAD_DIM), 0.02),
        'attn_k_gain': 1.0 + _normal(ks[6], (N_ATTN_LAYERS, HEAD_DIM), 0.02),
        'attn_w_out': _normal(ks[7], (N_ATTN_LAYERS, N_Q_HEADS * HEAD_DIM, D), DN_BETA * (N_Q_HEADS * HEAD_DIM) ** -0.5),
        'ml_w_in': _normal(ks[8], (N_MLSTM_LAYERS, D, ML_IN), D ** -0.5),
        'ml_b_gate': gate_base + _normal(ks[9], (N_MLSTM_LAYERS, 4, ML_HEADS), 0.1),
        'ml_head_gain': 1.0 + _normal(ks[10], (N_MLSTM_LAYERS, ML_HEADS, ML_DV), 0.02),
        'ml_w_out': _normal(ks[11], (N_MLSTM_LAYERS, D, D), DN_BETA * D ** -0.5),
        'xa_w_q': _normal(ks[12], (DEPTH, D, D), D ** -0.5),
        'xa_w_kv': _normal(ks[13], (DEPTH, D, 2 * D), D ** -0.5),
        'xa_w_out': _normal(ks[14], (DEPTH, D, D), DN_BETA * D ** -0.5),
        'mlp_w1': _normal(ks[15], (DEPTH, D, D_FF), D ** -0.5),
        'mlp_w2': _normal(ks[16], (DEPTH, D_FF, D), DN_BETA * D_FF ** -0.5),
        'ln_g': 1.0 + _normal(ks[17], (DEPTH, 3, D), 0.02),
        'ln_b': _normal(ks[18], (DEPTH, 3, D), 0.02),
    }


def reference(x_prompt, x_sample, mem_prompt, mem_sample, attn_w_in, attn_q_gain, attn_k_gain, attn_w_out,
              ml_w_in, ml_b_gate, ml_head_gain, ml_w_out, xa_w_q, xa_w_kv, xa_w_out, mlp_w1, mlp_w2,
              ln_g, ln_b):
    params = {
        'attn_w_in': attn_w_in, 'attn_q_gain': attn_q_gain, 'attn_k_gain': attn_k_gain,
        'attn_w_out': attn_w_out, 'ml_w_in': ml_w_in, 'ml_b_gate': ml_b_gate,
        'ml_head_gain': ml_head_gain, 'ml_w_out': ml_w_out, 'xa_w_q': xa_w_q, 'xa_w_kv': xa_w_kv,
        'xa_w_out': xa_w_out, 'mlp_w1': mlp_w1, 'mlp_w2': mlp_w2, 'ln_g': ln_g, 'ln_b': ln_b,
    }
    y_prompt = _trunk(x_prompt, mem_prompt, params)
    y_sample = _trunk(x_sample, mem_sample, params)
    return (y_prompt, y_sample)
```

```python
import numpy as np
from contextlib import ExitStack
import concourse.bass as bass
import concourse.mybir as mybir
from concourse.bass_utils import run_bass_kernel_spmd

F32 = mybir.dt.float32
BF16 = mybir.dt.bfloat16
AF = mybir.ActivationFunctionType
ALU = mybir.AluOpType
AX = mybir.AxisListType

D = 2048
NTOK = 4096
BT = 512
NB = NTOK // BT
KC = 16
DEPTH = 4
DFF = 8192
ATTN_IN = 3072
ML_IN = 6176
DN_ALPHA = (2 * DEPTH) ** 0.25
LN_EPS = 1e-5
RMS_EPS = 1e-6
NEG = -30000.0

W_SHAPES = {
    'attn_w_in': (2, D, ATTN_IN), 'attn_w_out': (2, D, D), 'ml_w_in': (2, D, ML_IN), 'ml_w_out': (2, D, D),
    'xa_w_q': (4, D, D), 'xa_w_kv': (4, D, 2 * D), 'xa_w_out': (4, D, D),
    'mlp_w1': (4, D, DFF), 'mlp_w2': (4, DFF, D),
}


def _slab_table():
    tab = {}
    n = 0
    order = []
    loc = []
    cnt = [0] * DEPTH
    for layer in range(DEPTH):
        j = layer // 2
        names = ['xa_w_kv']
        names += (['attn_w_in', 'attn_w_out'] if layer % 2 == 0 else ['ml_w_in', 'ml_w_out'])
        names += ['xa_w_q', 'xa_w_out', 'mlp_w1', 'mlp_w2']
        for nm in names:
            li = j if nm.startswith(('attn', 'ml_')) else layer
            _, R, C = W_SHAPES[nm]
            for rq in range(R // 2048):
                for cq in range((C + 511) // 512):
                    w = min(512, C - cq * 512)
                    tab[(nm, li, rq, cq)] = (n, w)
                    order.append((nm, li, rq, cq, n, w))
                    loc.append((layer, cnt[layer]))
                    cnt[layer] += 1
                    n += 1
    return tab, order, n, loc, cnt


SLAB_TAB, SLAB_ORDER, NSLAB, SLAB_LOC, SLAB_CNT = _slab_table()


class Buf:
    __slots__ = ('t', 'w', 'r', 'rp', 'name', 'excl')

    def __init__(self, t, name=''):
        self.t = t
        self.excl = False
        self.w = {}
        self.r = {}
        self.rp = {}
        self.name = name


def _upd(d, key, val):
    if d.get(key, 0) < val:
        d[key] = val


class Eng:
    def __init__(self, k, eng, name):
        self.k = k
        self.eng = eng
        self.name = name
        self.sem = k.newsem('e_' + name)
        self.cnt = 0
        self.seen = {}

    def wait(self, tok):
        key, val = tok
        if self.seen.get(key, 0) >= val:
            return
        self.eng.wait_ge(self.k.sems[key], val)
        self.seen[key] = val

    def acquire(self, reads=(), writes=(), nowaw=False):
        self._nowaw = nowaw
        for b in reads:
            for kv in b.w.items():
                self.wait(kv)
            if b.excl:
                for kv in b.r.items():
                    if kv[0] != self.sem:
                        self.wait(kv)
        for b in writes:
            if not nowaw:
                for kv in b.w.items():
                    self.wait(kv)
            for kv in b.r.items():
                self.wait(kv)
            for kv in b.rp.items():
                self.wait(kv)

    def release(self, ins, reads=(), writes=()):
        self.cnt += 1
        ins.then_inc(self.k.sems[self.sem], 1)
        for b in reads:
            _upd(b.r, self.sem, self.cnt)
        for b in writes:
            _upd(b.w, self.sem, self.cnt)
            if not self._nowaw:
                b.rp = b.r
                b.r = {}
        return (self.sem, self.cnt)

    def op(self, fn, reads, writes, *a, nowaw=False, **kw):
        self.acquire(reads, writes, nowaw)
        ins = fn(*a, **kw)
        return self.release(ins, reads, writes)


class DmaQ:
    def __init__(self, k, E, nsem, name):
        self.k = k
        self.E = E
        self.pool = [k.newsem('d_%s%d' % (name, i)) for i in range(nsem)]
        self.cnt = [0] * nsem
        self.i = 0

    def dma(self, out, in_, reads=(), writes=(), nowaw=False):
        E = self.E
        E.acquire(reads, writes, nowaw)
        j = self.i
        self.i = (self.i + 1) % len(self.pool)
        key = self.pool[j]
        if self.cnt[j]:
            E.wait((key, self.cnt[j]))
        ins = E.eng.dma_start(out=out, in_=in_)
        self.cnt[j] += 16
        ins.then_inc(self.k.sems[key], 16)
        for b in reads:
            _upd(b.r, key, self.cnt[j])
        for b in writes:
            _upd(b.w, key, self.cnt[j])
            if not nowaw:
                b.rp = b.r
                b.r = {}
        return (key, self.cnt[j])


class K:
    def __init__(self, nlayers=DEPTH, debug=False, ml_mode=3):
        self.nlayers = nlayers
        self.debug = debug
        self.ml_mode = ml_mode
        self.probe = False
        self.stop = 99
        self.nc = bass.Bass("TRN2", target_bir_lowering=False)
        self.es = ExitStack()
        self.sems = {}
        self.nsem = 0
        nc = self.nc
        self.PE = Eng(self, nc.tensor, 'pe')
        self.ACT = Eng(self, nc.scalar, 'act')
        self.DVE = Eng(self, nc.vector, 'dve')
        self.POOL = Eng(self, nc.gpsimd, 'pool')
        self.SP = Eng(self, nc.sync, 'sp')
        self.qld = DmaQ(self, self.SP, 12, 'ld')
        self.qst = DmaQ(self, self.ACT, 8, 'st')
        self.qcv = DmaQ(self, self.POOL, 8, 'cv')
        self.psrr = 0

    def newsem(self, name):
        h = self.es.enter_context(self.nc.semaphore(name))
        key = self.nsem
        self.nsem += 1
        self.sems[key] = h
        return key

    def sb(self, name, shape, dt):
        return Buf(self.es.enter_context(self.nc.sbuf_tensor('s_' + name, shape, dt)), name)

    def din(self, name, shape, dt=F32):
        return self.nc.dram_tensor(name, shape, dt, kind="ExternalInput").ap()

    def dscr(self, name, shape, dt):
        return Buf(self.nc.dram_tensor(name, shape, dt, kind="Internal").ap(), name)

    def bank(self, n=1):
        if n == 1:
            i = self.psrr
            self.psrr = (self.psrr + 1) % 8
            return [self.PSB[i]], self.ps[:, i, :]
        if self.psrr % 2:
            self.psrr = (self.psrr + 1) % 8
        i = self.psrr
        self.psrr = (self.psrr + 2) % 8
        return [self.PSB[i], self.PSB[i + 1]], self.ps[:, i:i + 2, :]

    def build(self):
        nc = self.nc
        PE, ACT, DVE, POOL, SP = self.PE, self.ACT, self.DVE, self.POOL, self.SP
        self.x_in = self.din('x', [NTOK, D])
        self.mem_in = self.din('mem', [512, D])
        self.flags_in = self.din('flags', [128, 4])
        self.cst_in = self.din('cst', [128, 6, 128])
        self.rope_in = self.din('rope', [2, 128, NTOK])
        self.lng_in = self.din('lng', [128, 12 * KC])
        self.lnb_in = self.din('lnb', [128, 12 * KC])
        self.qkg_in = self.din('qkg', [128, 4])
        self.mlb_in = self.din('mlb', [128, 2 * 32])
        self.mlwg_in = self.din('mlwg', [2, 128, KC * 32])
        self.mlg_in = self.din('mlg', [128, 2 * KC])
        self.w_in = {}
        for nm, shp in W_SHAPES.items():
            lead = shp[0] if self.nlayers == DEPTH else max(1, ((self.nlayers + 1) // 2 if nm.startswith('attn') else
                                                              (self.nlayers // 2 if nm.startswith('ml_') else self.nlayers)))
            self.w_in[nm] = self.din(nm, [lead] + list(shp[1:])) if not self.probe else None
        self.y_out = self.nc.dram_tensor('y', [NTOK, D], F32, kind="ExternalOutput").ap()
        if self.debug:
            self.dbg_out = [self.nc.dram_tensor('dbg%d' % i, [128, KC, NTOK], F32, kind="ExternalOutput").ap()
                            for i in range(3)]
            self.dbgB = Buf(None, 'dbgB')
        self.wbl = [self.nc.dram_tensor('wb%d' % l, [SLAB_CNT[l], 128, KC * 512], BF16, kind="Internal").ap()
                    for l in range(self.nlayers)]
        self.wbB = [Buf(None, 'wb%d' % i) for i in range(NSLAB)]
        self.xT = [Buf(None, 'xT%d' % b) for b in range(NB)]
        self.xT_ap = self.nc.dram_tensor('xT', [128, KC, NTOK], F32, kind="Internal").ap()
        self.qT_ap = self.nc.dram_tensor('qT', [128, 16, NTOK], BF16, kind="Internal").ap()
        self.kT_ap = self.nc.dram_tensor('kT', [128, 8, NTOK], BF16, kind="Internal").ap()
        self.v_ap = self.nc.dram_tensor('vv', [NTOK, D], BF16, kind="Internal").ap()
        self.ktok_ap = self.nc.dram_tensor('ktok', [NTOK, 1024], BF16, kind="Internal").ap()
        self.o_ap = self.nc.dram_tensor('og', [NTOK, D], F32, kind="Internal").ap()
        self.g_ap = self.nc.dram_tensor('gts', [NB, 128, 128], F32, kind="Internal").ap()
        self.wg_ap = self.nc.dram_tensor('wg', [2, 128, KC * 32], BF16, kind="Internal").ap()
        self.wgB = [Buf(None, 'wgB%d' % i) for i in range(2)]
        self.hf_ap = self.nc.dram_tensor('hf', [NTOK, D], F32, kind="Internal").ap()
        self.kmT_ap = self.nc.dram_tensor('kmT', [DEPTH, 128, KC, 512], BF16, kind="Internal").ap()
        self.vm_ap = self.nc.dram_tensor('vm', [DEPTH, 512, D], BF16, kind="Internal").ap()
        self.qTB = [Buf(None) for _ in range(NB)]
        self.kTB = [Buf(None) for _ in range(NB)]
        self.vB = [Buf(None) for _ in range(NB)]
        self.ktokB = [Buf(None) for _ in range(NB)]
        self.oB = [Buf(None) for _ in range(NB)]
        self.gB = [Buf(None) for _ in range(NB)]
        self.hfB = [Buf(None) for _ in range(NB)]
        self.kmTB = [Buf(None) for _ in range(DEPTH)]
        self.vmB = [Buf(None) for _ in range(DEPTH)]
        self.slab = [self.sb('slab%d' % i, [128, KC, 512], BF16) for i in range(3)]
        self.slab_i = 0
        self.x32 = self.sb('x32', [128, KC, BT], F32)
        self.x32c = [Buf(None, 'x32c%d' % i) for i in range(KC)]
        self.xTb = self.sb('xTb', [128, KC, BT], BF16)
        self.OT = self.sb('OT', [128, KC, BT], BF16)
        self.U = self.es.enter_context(nc.sbuf_tensor('s_U', [128, 32768], BF16))
        self.cst = self.sb('cst', [128, 6, 128], F32)
        self.identb = self.sb('identb', [128, 128], BF16)
        self.onesb = self.sb('onesb', [128, 128], BF16)
        self.onesf = self.sb('onesf', [128, 128], F32)
        self.flags = self.sb('flags', [128, 4], F32)
        self.lng = self.sb('lng', [128, 12 * KC], F32)
        self.lnb = self.sb('lnb', [128, 12 * KC], F32)
        self.qkg = self.sb('qkg', [128, 4], F32)
        self.mlb = self.sb('mlb', [128, 64], F32)
        self.mlg = self.sb('mlg', [128, 2 * KC], F32)
        self.gw = self.sb('gw', [128, KC, 32], BF16)
        self.C32 = self.sb('C32', [128, 8, 256], F32)
        self.Cb = self.sb('Cb', [128, 8, 256], BF16)
        self.n32 = self.sb('n32', [128, 8], F32)
        self.nb = self.sb('nb', [128, 8, 2], BF16)
        self.sm = [self.sb('sm%d' % i, [128, 8], F32) for i in range(8)]
        self.sm_i = 0
        self.t32 = [self.sb('t32_%d' % i, [128, 512], F32) for i in range(4)]
        self.t32_i = 0
        self.tb16 = [self.sb('tb16_%d' % i, [128, 2, 512], BF16) for i in range(2)]
        self.tb16_i = 0
        self.ps = self.es.enter_context(nc.psum_tensor('p_ps', [128, 8, 512], F32))
        self.PSB = [Buf(None, 'psb%d' % i) for i in range(8)]
        for pb_ in self.PSB:
            pb_.excl = True
        self.yB = Buf(None, 'yB')
        U = self.U
        self.hT = Buf(U[:, :].rearrange("p (c t) -> p c t", t=BT), 'hT')
        self.KT = Buf(U[:, 0:16384].rearrange("p (h t) -> p h t", t=NTOK), 'KT')
        self.Vv = Buf(U[:, 16384:32768].rearrange("p (k d) -> p k d", d=512), 'Vv')
        self.zb = Buf(U[:, 0:8192].rearrange("p (c t) -> p c t", t=BT), 'zb')
        self.zsq = Buf(U[:, 8192:16384].rearrange("p (c t) -> p c t", t=BT), 'zsq')
        self.qx = Buf(U[:, 0:8192].rearrange("p (c t) -> p c t", t=BT), 'qx')
        self.kmT = Buf(U[:, 8192:12288].rearrange("p (c m) -> p c m", m=256), 'kmT')
        self.vmS = Buf(U[:, 12288:16384].rearrange("p (m d) -> p m d", d=D), 'vmS')
        self.PTx = Buf(U[:, 16384:20480].rearrange("p (g t) -> p g t", t=BT), 'PTx')
        self.qkst = Buf(U[:, 0:10240].rearrange("p (h t) -> p h t", t=BT), 'qkst')
        self.vst = Buf(U[:, 10240:12288].rearrange("p (a d) -> p a d", d=512), 'vst')
        self.stage32 = Buf(U[:, 12288:16384].bitcast(F32), 'stage32')
        self.vmst = Buf(U[:, 0:8192].rearrange("p (t d) -> p t d", d=D), 'vmst')
        self.ropeC = Buf(U[:, 16384:17408].bitcast(F32), 'ropeC')
        self.ropeS = Buf(U[:, 17408:18432].bitcast(F32), 'ropeS')
        self.qTc = Buf(U[:, 0:4096].rearrange("p (h t) -> p h t", t=BT), 'qTc')
        self.kTc = Buf(U[:, 4096:8192].rearrange("p (h t) -> p h t", t=BT), 'kTc')
        self.ktk = Buf(U[:, 8192:12288].rearrange("p (c d) -> p c d", d=1024), 'ktk')
        self.vau = [Buf(U[:, 12288 + i * 2048:12288 + (i + 1) * 2048], 'vau%d' % i) for i in range(2)]
        self.gts = Buf(U[:, 16384:16640].bitcast(F32).rearrange("p (c g) -> p c g", g=32), 'gts')
        self.hbuf = [Buf(U[:, 16640 + i * 4096:16640 + (i + 1) * 4096].bitcast(F32), 'hbuf%d' % i) for i in range(2)]
        self.obuf = Buf(U[:, 24832:28928].bitcast(F32), 'obuf')
        self.gst = Buf(U[:, 28928:29184].bitcast(F32).rearrange("p (c g) -> p c g", g=32), 'gst')
        self.mq8 = Buf(U[:, 0:4096].rearrange("p (h t) -> p h t", t=BT), 'mq8')
        self.mk8 = Buf(U[:, 4096:8192].rearrange("p (h t) -> p h t", t=BT), 'mk8')
        self.mkt = Buf(U[:, 8192:12288].rearrange("p (c d) -> p c d", d=1024), 'mkt')
        self.mvs = Buf(U[:, 12288:20480].rearrange("p (c d) -> p c d", d=D), 'mvs')
        self.ugroup = [self.ropeC, self.ropeS, self.qTc, self.kTc, self.ktk, self.vau[0], self.vau[1], self.gts,
                       self.hbuf[0], self.hbuf[1], self.obuf, self.gst, self.mq8, self.mk8, self.mkt, self.mvs,
                       self.stage32, self.vmst, self.hT, self.KT, self.Vv, self.zb, self.zsq, self.qx, self.kmT, self.vmS, self.PTx,
                       self.qkst, self.vst]

        ld = self.qld.dma
        ld(self.cst.t[:], self.cst_in, writes=[self.cst])
        ld(self.flags.t[:], self.flags_in, writes=[self.flags])
        ld(self.lng.t[:], self.lng_in, writes=[self.lng])
        ld(self.lnb.t[:], self.lnb_in, writes=[self.lnb])
        ld(self.qkg.t[:], self.qkg_in, writes=[self.qkg])
        ld(self.mlb.t[:], self.mlb_in, writes=[self.mlb])
        ld(self.mlg.t[:], self.mlg_in, writes=[self.mlg])
        DVE.op(DVE.eng.tensor_copy, [self.cst], [self.identb], out=self.identb.t[:], in_=self.cst.t[:, 0, :])
        DVE.op(DVE.eng.memset, [], [self.onesb], self.onesb.t[:], 1.0)
        DVE.op(DVE.eng.memset, [], [self.onesf], self.onesf.t[:], 1.0)
        self.epsc = self.sb('epsc', [128, 4], F32)
        DVE.op(DVE.eng.memset, [], [self.epsc], self.epsc.t[:, 0:1], 128.0 * RMS_EPS)
        DVE.op(DVE.eng.memset, [], [self.epsc], self.epsc.t[:, 1:2], LN_EPS)
        DVE.op(DVE.eng.memset, [], [self.epsc], self.epsc.t[:, 2:3], RMS_EPS)
        DVE.op(DVE.eng.memset, [], [self.epsc], self.epsc.t[:, 3:4], 1.0)
        DVE.op(DVE.eng.tensor_scalar_mul, [self.qkg], [self.qkg], out=self.qkg.t[:], in0=self.qkg.t[:],
               scalar1=float(128.0 ** 0.5))

        if self.probe:
            return nc
        for (nm, li, rq, cq, n, w) in SLAB_ORDER:
            if (nm.startswith(('attn', 'ml_')) and 2 * li + (0 if nm.startswith('attn') else 1) >= self.nlayers) or \
               (not nm.startswith(('attn', 'ml_')) and li >= self.nlayers):
                continue
            src = self.w_in[nm][li, rq * 2048:(rq + 1) * 2048, cq * 512:cq * 512 + w]
            if w == 32:
                continue
            dst = self.wbs(n).rearrange("p (k c) -> p k c", c=512)[:, :, 0:w]
            self.qcv.dma(dst, src.rearrange("(k p) c -> p k c", p=128), writes=[self.wbB[n]])

        self.prelude_mem()
        for layer in range(self.nlayers):
            if layer % 2 == 0:
                self.attn_phase1(layer)
                order = list(range(NB))
            else:
                self.ml_phase1(layer)
                if self.ml_mode >= 2:
                    self.ml_fwd(layer)
                order = list(range(NB - 1, -1, -1))
            for b in order:
                if layer % 2 == 1 and self.ml_mode < 3:
                    self.load_x_block(layer, b)
                    self.finish_block(layer, b)
                    continue
                if layer % 2 == 0:
                    self.attn_core(layer, b)
                    self.dense_resid(('attn_w_out', layer // 2), self.OT, layer, 0, b)
                else:
                    self.ml_bwd_block(layer, b)
                    self.dense_resid(('ml_w_out', layer // 2), self.OT, layer, 0, b)
                self.dbg_dump(0, layer, b)
                self.xattn(layer, b)
                self.dense_resid(('xa_w_out', layer), self.OT, layer, 1, b)
                self.dbg_dump(1, layer, b)
                self.mlp(layer, b)
                self.dbg_dump(2, layer, b)
                self.finish_block(layer, b)
        self.drain()
        return nc

    def drain(self):
        for q in (self.qst, self.qld, self.qcv):
            for j, key in enumerate(q.pool):
                if q.cnt[j]:
                    q.E.wait((key, q.cnt[j]))

    def dbg_dump(self, i, layer, b):
        if self.debug and layer == self.nlayers - 1:
            self.qst.dma(self.dbg_out[i][:, :, b * BT:(b + 1) * BT], self.x32.t[:], reads=self.x32c, writes=[self.dbgB])

    def tmp32(self):
        b = self.t32[self.t32_i]
        self.t32_i = (self.t32_i + 1) % len(self.t32)
        return b

    def tmpb(self):
        b = self.tb16[self.tb16_i]
        self.tb16_i = (self.tb16_i + 1) % len(self.tb16)
        return b

    def take(self, newbufs):
        toks = {}
        for b in self.ugroup:
            if b in newbufs:
                continue
            for key, val in list(b.w.items()) + list(b.r.items()) + list(b.rp.items()):
                _upd(toks, key, val)
        for nb_ in newbufs:
            for key, val in toks.items():
                _upd(nb_.r, key, val)

    def wbs(self, n):
        l, i = SLAB_LOC[n]
        return self.wbl[l][i]

    def load_slab(self, nm, li, rq, cq):
        n, w = SLAB_TAB[(nm, li, rq, cq)]
        s = self.slab[self.slab_i]
        self.slab_i = (self.slab_i + 1) % len(self.slab)
        self.qld.dma(s.t[:, :, 0:w], self.wbs(n).rearrange("p (k c) -> p k c", c=512)[:, :, 0:w],
                     reads=[self.wbB[n]], writes=[s])
        return s

    def mm_group(self, out_ap, pairs, reads, writes):
        PE = self.PE
        PE.acquire(reads, writes)
        n = len(pairs)
        ins = None
        for i, (l, r) in enumerate(pairs):
            ins = PE.eng.matmul(out=out_ap, lhsT=l, rhs=r, start=(i == 0), stop=(i == n - 1))
        return PE.release(ins, reads, writes)

    def bank_at(self, i, n=1):
        if n == 1:
            return [self.PSB[i]], self.ps[:, i, :]
        return [self.PSB[i + k] for k in range(n)], self.ps[:, i:i + n, :]

    def load_x_block(self, layer, b):
        self.qld.dma(self.x32.t[:], self.xT_ap[:, :, b * BT:(b + 1) * BT], reads=[self.xT[b]], writes=self.x32c)

    def transpose_in(self, src_rows, dst_t, dst_bufs):
        PE, ACT, DVE = self.PE, self.ACT, self.DVE
        stg = self.stage32
        for t in range(4):
            self.qld.dma(stg.t[:], src_rows[t * 128:(t + 1) * 128, :], writes=[stg])
            for g in range(4):
                pb, pap = self.bank()
                PE.acquire([stg, self.cst], pb)
                for i in range(4):
                    c = g * 4 + i
                    ins = PE.eng.transpose(out=pap[:, i * 128:(i + 1) * 128], in_=stg.t[:, c * 128:(c + 1) * 128],
                                           identity=self.cst.t[:, 0, :])
                PE.release(ins, [stg, self.cst], pb)
                wb_ = dst_bufs[4 * g:4 * g + 4]
                if g % 2 == 0:
                    ACT.op(ACT.eng.copy, pb, wb_, nowaw=True, out=dst_t[:, g * 4:(g + 1) * 4, t * 128:(t + 1) * 128],
                           in_=pap.rearrange("p (i t) -> p i t", t=128))
                else:
                    DVE.op(DVE.eng.tensor_copy, pb, wb_, nowaw=True,
                           out=dst_t[:, g * 4:(g + 1) * 4, t * 128:(t + 1) * 128],
                           in_=pap.rearrange("p (i t) -> p i t", t=128))

    def cast_x(self):
        DVE, ACT = self.DVE, self.ACT
        for q in range(4):
            E = DVE if q % 2 == 0 else ACT
            fn = DVE.eng.tensor_copy if q % 2 == 0 else ACT.eng.copy
            E.op(fn, self.x32c[4 * q:4 * q + 4], [self.xTb], nowaw=(q > 0),
                 out=self.xTb.t[:, 4 * q:4 * q + 4, :], in_=self.x32.t[:, 4 * q:4 * q + 4, :])

    def prelude_mem(self):
        ACT, DVE = self.ACT, self.DVE
        self.take([self.stage32])
        self.transpose_in(self.mem_in, self.x32.t, self.x32c)
        self.cast_x()
        self.take([self.vmst])
        for layer in range(self.nlayers):
            for cq in range(4):
                s = self.load_slab('xa_w_kv', layer, 0, cq)
                for jj in range(4):
                    c = cq * 4 + jj
                    pb, pap = self.bank()
                    self.mm_group(pap, [(s.t[:, kc, jj * 128:(jj + 1) * 128], self.xTb.t[:, kc, :]) for kc in range(KC)],
                                  [s, self.xTb], pb)
                    ACT.op(ACT.eng.copy, pb, [self.OT], nowaw=(c > 0), out=self.OT.t[:, c, :], in_=pap)
            self.qst.dma(self.kmT_ap[layer], self.OT.t[:], reads=[self.OT], writes=[self.kmTB[layer]])
            for cq in range(4):
                s = self.load_slab('xa_w_kv', layer, 0, 4 + cq)
                for t in range(4):
                    pb, pap = self.bank()
                    self.mm_group(pap, [(self.xTb.t[:, kc, t * 128:(t + 1) * 128], s.t[:, kc, :]) for kc in range(KC)],
                                  [s, self.xTb], pb)
                    DVE.op(DVE.eng.tensor_copy, pb, [self.vmst], nowaw=(cq + t > 0),
                           out=self.vmst.t[:, t, cq * 512:(cq + 1) * 512], in_=pap)
            self.qst.dma(self.vm_ap[layer].rearrange("(t p) d -> p t d", p=128), self.vmst.t[:],
                         reads=[self.vmst], writes=[self.vmB[layer]])

    def qk_post(self, pb, pap, h, gcol):
        ACT, DVE = self.ACT, self.DVE
        sq = self.tmpb()
        ACT.op(ACT.eng.activation, pb, [sq], out=sq.t[:, 0, :], in_=pap, func=AF.Square)
        pb2, pap2 = self.bank()
        self.mm_group(pap2, [(self.onesb.t[:], sq.t[:, 0, :])], [self.onesb, sq], pb2)
        rstd = self.tmp32()
        ACT.op(ACT.eng.activation, pb2 + [self.epsc], [rstd], out=rstd.t[:], in_=pap2, func=AF.Sqrt, bias=self.epsc.t[:, 0:1])
        DVE.op(DVE.eng.reciprocal, [rstd], [rstd], out=rstd.t[:], in_=rstd.t[:])
        qn = self.tmp32()
        DVE.op(DVE.eng.scalar_tensor_tensor, pb + [self.qkg, rstd], [qn], out=qn.t[:], in0=pap,
               scalar=self.qkg.t[:, gcol:gcol + 1], in1=rstd.t[:], op0=ALU.mult, op1=ALU.mult)
        pb3, pap3 = self.bank()
        self.mm_group(pap3, [(self.cst.t[:, 1, :], qn.t[:])], [self.cst, qn], pb3)
        t1 = self.tmp32()
        DVE.op(DVE.eng.tensor_tensor, [qn, self.ropeC], [t1], out=t1.t[:], in0=qn.t[:], in1=self.ropeC.t[:], op=ALU.mult)
        t2 = self.tmp32()
        DVE.op(DVE.eng.tensor_tensor, pb3 + [self.ropeS], [t2], out=t2.t[:], in0=pap3, in1=self.ropeS.t[:], op=ALU.mult)
        DVE.op(DVE.eng.tensor_tensor, [t1, t2], [self.qkst], nowaw=True, out=self.qkst.t[:, h, :], in0=t1.t[:], in1=t2.t[:],
               op=ALU.add)

    def attn_phase1(self, layer):
        j = layer // 2
        ACT, DVE = self.ACT, self.DVE
        self.take([self.qkst, self.vst, self.stage32, self.ropeC, self.ropeS])
        for b in range(NB):
            tsl = slice(b * BT, (b + 1) * BT)
            if layer == 0:
                self.transpose_in(self.x_in[tsl, :], self.x32.t, self.x32c)
                self.qst.dma(self.xT_ap[:, :, tsl], self.x32.t[:], reads=self.x32c, writes=[self.xT[b]])
            else:
                self.load_x_block(layer, b)
            self.cast_x()
            self.qld.dma(self.ropeC.t[:], self.rope_in[0, :, tsl], writes=[self.ropeC])
            self.qld.dma(self.ropeS.t[:], self.rope_in[1, :, tsl], writes=[self.ropeS])
            for cq in range(5):
                s = self.load_slab('attn_w_in', j, 0, cq)
                for jj in range(4):
                    h = cq * 4 + jj
                    pb, pap = self.bank()
                    self.mm_group(pap, [(s.t[:, kc, jj * 128:(jj + 1) * 128], self.xTb.t[:, kc, :]) for kc in range(KC)],
                                  [s, self.xTb], pb)
                    self.qk_post(pb, pap, h, j if cq < 4 else 2 + j)
            s = self.load_slab('attn_w_in', j, 0, 5)
            for t in range(4):
                pb, pap = self.bank()
                self.mm_group(pap, [(self.xTb.t[:, kc, t * 128:(t + 1) * 128], s.t[:, kc, :]) for kc in range(KC)],
                              [s, self.xTb], pb)
                ACT.op(ACT.eng.copy, pb, [self.vst], nowaw=(t > 0), out=self.vst.t[:, t, :], in_=pap)
            self.qst.dma(self.qT_ap[:, :, tsl], self.qkst.t[:, 0:16, :], reads=[self.qkst], writes=[self.qTB[b]])
            self.qst.dma(self.kT_ap[:, 0:4, tsl], self.qkst.t[:, 16:20, :], reads=[self.qkst], writes=[self.kTB[b]])
            self.qst.dma(self.v_ap[tsl, 0:512].rearrange("(t p) d -> p t d", p=128), self.vst.t[:],
                         reads=[self.vst], writes=[self.vB[b]])

    def attn_core(self, layer, b):
        PE, ACT, DVE = self.PE, self.ACT, self.DVE
        tsl = slice(b * BT, (b + 1) * BT)
        self.take([self.KT, self.Vv])
        self.qld.dma(self.KT.t[:], self.kT_ap[:, 0:4, :], reads=self.kTB, writes=[self.KT])
        self.qld.dma(self.Vv.t[:], self.v_ap[:, 0:512].rearrange("(k p) d -> p k d", p=128), reads=self.vB,
                     writes=[self.Vv])
        self.load_x_block(layer, b)
        qb = self.xTb
        self.qld.dma(qb.t[:], self.qT_ap[:, :, tsl], reads=[self.qTB[b]], writes=[qb])
        scale = float(128.0 ** -0.5)
        NG = 16
        for h in range(16):
            kvh = h // 4
            ob, oap = self.bank_at(4 + 2 * (h % 2))
            sb_, sap = self.bank_at(5 + 2 * (h % 2))

            def scores(g):
                pb, pap = self.bank_at(2 * (g % 2), 2)
                PE.acquire([self.KT, qb], pb)
                for i in range(2):
                    kt = 2 * g + i
                    ins = PE.eng.matmul(out=pap[:, i, :], lhsT=self.KT.t[:, kvh, kt * 128:(kt + 1) * 128],
                                        rhs=qb.t[:, h, :], start=True, stop=True)
                PE.release(ins, [self.KT, qb], pb)
                return pb, pap

            cur = scores(0)
            for g in range(NG):
                nxt = scores(g + 1) if g + 1 < NG else None
                pb, pap = cur
                PT = self.tmpb()
                mcol = 1 if (g // 8) != (b // 4) else 0
                ACT.op(ACT.eng.activation, pb + [self.flags], [PT], out=PT.t[:], in_=pap, func=AF.Exp,
                       bias=self.flags.t[:, mcol:mcol + 1], scale=scale)
                PE.acquire([PT, self.Vv, self.onesb], ob + sb_, nowaw=(g > 0))
                for i in range(2):
                    kt = 2 * g + i
                    PE.eng.matmul(out=oap, lhsT=self.Vv.t[:, kt, kvh * 128:(kvh + 1) * 128], rhs=PT.t[:, i, :],
                                  start=(g == 0 and i == 0), stop=(g == NG - 1 and i == 1))
                    ins = PE.eng.matmul(out=sap, lhsT=self.onesb.t[:], rhs=PT.t[:, i, :],
                                        start=(g == 0 and i == 0), stop=(g == NG - 1 and i == 1))
                PE.release(ins, [PT, self.Vv, self.onesb], ob + sb_)
                cur = nxt
            rec = self.tmp32()
            DVE.op(DVE.eng.reciprocal, sb_, [rec], out=rec.t[:], in_=sap)
            DVE.op(DVE.eng.tensor_tensor, ob + [rec], [self.OT], nowaw=(h > 0), out=self.OT.t[:, h, :], in0=oap,
                   in1=rec.t[:], op=ALU.mult)

    def dense_resid(self, wkey, src, layer, lnidx, b):
        DVE = self.DVE
        nm, li = wkey
        for cq in range(4):
            s = self.load_slab(nm, li, 0, cq)
            for jj in range(4):
                c = cq * 4 + jj
                pb, pap = self.bank()
                self.mm_group(pap, [(s.t[:, kc, jj * 128:(jj + 1) * 128], src.t[:, kc, :]) for kc in range(KC)],
                              [s, src], pb)
                DVE.op(DVE.eng.scalar_tensor_tensor, pb + [self.x32c[c]], [self.x32c[c]], out=self.x32.t[:, c, :],
                       in0=self.x32.t[:, c, :], scalar=float(DN_ALPHA), in1=pap, op0=ALU.mult, op1=ALU.add)
        self.layer_norm(layer, lnidx)

    def layer_norm(self, layer, lnidx):
        PE, ACT, DVE = self.PE, self.ACT, self.DVE
        self.take([self.zb, self.zsq])
        for q in range(4):
            cs = slice(4 * q, 4 * q + 4)
            ACT.op(ACT.eng.copy, self.x32c[cs], [self.zb], nowaw=(q > 0), out=self.zb.t[:, cs, :], in_=self.x32.t[:, cs, :])
            ACT.op(ACT.eng.activation, self.x32c[cs], [self.zsq], nowaw=(q > 0), out=self.zsq.t[:, cs, :],
                   in_=self.x32.t[:, cs, :], func=AF.Square)
        ab, aap = self.bank()
        bb, bap = self.bank()
        self.mm_group(aap, [(self.onesb.t[:], self.zb.t[:, kc, :]) for kc in range(KC)], [self.onesb, self.zb], ab)
        self.mm_group(bap, [(self.onesb.t[:], self.zsq.t[:, kc, :]) for kc in range(KC)], [self.onesb, self.zsq], bb)
        mean = self.tmp32()
        DVE.op(DVE.eng.tensor_scalar, ab, [mean], out=mean.t[:], in0=aap, scalar1=1.0 / D, scalar2=None, op0=ALU.mult)
        msq = self.tmp32()
        DVE.op(DVE.eng.tensor_tensor, [mean], [msq], out=msq.t[:], in0=mean.t[:], in1=mean.t[:], op=ALU.mult)
        var = self.tmp32()
        DVE.op(DVE.eng.scalar_tensor_tensor, bb + [msq], [var], out=var.t[:], in0=bap, scalar=1.0 / D, in1=msq.t[:],
               op0=ALU.mult, op1=ALU.subtract)
        rstd = self.tmp32()
        ACT.op(ACT.eng.activation, [var, self.epsc], [rstd], out=rstd.t[:], in_=var.t[:], func=AF.Sqrt, bias=self.epsc.t[:, 1:2])
        DVE.op(DVE.eng.reciprocal, [rstd], [rstd], out=rstd.t[:], in_=rstd.t[:])
        nmr = self.tmp32()
        DVE.op(DVE.eng.scalar_tensor_tensor, [mean, rstd], [nmr], out=nmr.t[:], in0=mean.t[:], scalar=-1.0, in1=rstd.t[:],
               op0=ALU.mult, op1=ALU.mult)
        gi = (layer * 3 + lnidx) * KC
        for c in range(KC):
            xc = self.x32.t[:, c, :]
            DVE.op(DVE.eng.tensor_tensor, [self.x32c[c], rstd], [self.x32c[c]], out=xc, in0=xc, in1=rstd.t[:], op=ALU.mult)
            DVE.op(DVE.eng.tensor_tensor, [self.x32c[c], nmr], [self.x32c[c]], out=xc, in0=xc, in1=nmr.t[:], op=ALU.add)
            ACT.op(ACT.eng.activation, [self.x32c[c], self.lng, self.lnb], [self.x32c[c]], out=xc, in_=xc,
                   func=AF.Identity, bias=self.lnb.t[:, gi + c:gi + c + 1], scale=self.lng.t[:, gi + c:gi + c + 1])
            ACT.op(ACT.eng.copy, [self.x32c[c]], [self.xTb], nowaw=(c > 0), out=self.xTb.t[:, c, :], in_=xc)

    def xattn(self, layer, b):
        PE, ACT, DVE = self.PE, self.ACT, self.DVE
        self.take([self.qx, self.kmT, self.vmS, self.PTx])
        slot = b // 4
        self.qld.dma(self.kmT.t[:], self.kmT_ap[layer][:, :, slot * 256:(slot + 1) * 256], reads=[self.kmTB[layer]],
                     writes=[self.kmT])
        self.qld.dma(self.vmS.t[:], self.vm_ap[layer][slot * 256:(slot + 1) * 256, :].rearrange("(m p) d -> p m d", p=128),
                     reads=[self.vmB[layer]], writes=[self.vmS])
        for cq in range(4):
            s = self.load_slab('xa_w_q', layer, 0, cq)
            for jj in range(4):
                c = cq * 4 + jj
                pb, pap = self.bank()
                self.mm_group(pap, [(s.t[:, kc, jj * 128:(jj + 1) * 128], self.xTb.t[:, kc, :]) for kc in range(KC)],
                              [s, self.xTb], pb)
                ACT.op(ACT.eng.copy, pb, [self.qx], nowaw=(c > 0), out=self.qx.t[:, c, :], in_=pap)
        scale = float(512.0 ** -0.5)
        for hx in range(4):
            for m in range(2):
                pb, pap = self.bank()
                self.mm_group(pap, [(self.kmT.t[:, hx * 4 + dc, m * 128:(m + 1) * 128], self.qx.t[:, hx * 4 + dc, :])
                                    for dc in range(4)], [self.kmT, self.qx], pb)
                ACT.op(ACT.eng.activation, pb, [self.PTx], nowaw=(hx + m > 0), out=self.PTx.t[:, hx * 2 + m, :], in_=pap,
                       func=AF.Exp, scale=scale)
        for hx in range(4):
            sb_, sap = self.bank()
            self.mm_group(sap, [(self.onesb.t[:], self.PTx.t[:, hx * 2 + m, :]) for m in range(2)],
                          [self.onesb, self.PTx], sb_)
            rec = self.tmp32()
            DVE.op(DVE.eng.reciprocal, sb_, [rec], out=rec.t[:], in_=sap)
            for dc in range(4):
                c = hx * 4 + dc
                pb, pap = self.bank()
                self.mm_group(pap, [(self.vmS.t[:, m, c * 128:(c + 1) * 128], self.PTx.t[:, hx * 2 + m, :])
                                    for m in range(2)], [self.vmS, self.PTx], pb)
                DVE.op(DVE.eng.tensor_tensor, pb + [rec], [self.OT], nowaw=(c > 0), out=self.OT.t[:, c, :], in0=pap,
                       in1=rec.t[:], op=ALU.mult)

    def mlp(self, layer, b):
        PE, ACT, DVE = self.PE, self.ACT, self.DVE
        self.take([self.hT])
        for cq in range(16):
            s = self.load_slab('mlp_w1', layer, 0, cq)
            for jj in range(4):
                c = cq * 4 + jj
                pb, pap = self.bank()
                self.mm_group(pap, [(s.t[:, kc, jj * 128:(jj + 1) * 128], self.xTb.t[:, kc, :]) for kc in range(KC)],
                              [s, self.xTb], pb)
                r = self.tmp32()
                ACT.op(ACT.eng.activation, pb, [r], out=r.t[:], in_=pap, func=AF.Relu)
                DVE.op(DVE.eng.tensor_tensor, [r], [self.hT], nowaw=(c > 0), out=self.hT.t[:, c, :], in0=r.t[:], in1=r.t[:],
                       op=ALU.mult)
        for cg in range(4):
            bks = [self.bank() for _ in range(4)]
            for rq in range(4):
                s = self.load_slab('mlp_w2', layer, rq, cg)
                for jj in range(4):
                    pb, pap = bks[jj]
                    PE.acquire([s, self.hT], pb, nowaw=(rq > 0))
                    for kc in range(KC):
                        ins = PE.eng.matmul(out=pap, lhsT=s.t[:, kc, jj * 128:(jj + 1) * 128],
                                            rhs=self.hT.t[:, rq * KC + kc, :],
                                            start=(rq == 0 and kc == 0), stop=(rq == 3 and kc == KC - 1))
                    PE.release(ins, [s, self.hT], pb)
            for jj in range(4):
                c = cg * 4 + jj
                pb, pap = bks[jj]
                DVE.op(DVE.eng.scalar_tensor_tensor, pb + [self.x32c[c]], [self.x32c[c]], out=self.x32.t[:, c, :],
                       in0=self.x32.t[:, c, :], scalar=float(DN_ALPHA), in1=pap, op0=ALU.mult, op1=ALU.add)
        self.layer_norm(layer, 2)

    def finish_block(self, layer, b):
        PE, ACT, DVE = self.PE, self.ACT, self.DVE
        tsl = slice(b * BT, (b + 1) * BT)
        if layer < self.nlayers - 1:
            self.qst.dma(self.xT_ap[:, :, tsl], self.x32.t[:], reads=self.x32c, writes=[self.xT[b]])
            return
        self.take([self.stage32])
        stg = self.stage32
        for t in range(4):
            for g in range(4):
                pb, pap = self.bank()
                PE.acquire(self.x32c[4 * g:4 * g + 4] + [self.cst], pb)
                for i in range(4):
                    c = g * 4 + i
                    ins = PE.eng.transpose(out=pap[:, i * 128:(i + 1) * 128], in_=self.x32.t[:, c, t * 128:(t + 1) * 128],
                                           identity=self.cst.t[:, 0, :])
                PE.release(ins, self.x32c[4 * g:4 * g + 4] + [self.cst], pb)
                if g % 2 == 0:
                    ACT.op(ACT.eng.copy, pb, [stg], nowaw=(g > 0), out=stg.t[:, g * 512:(g + 1) * 512], in_=pap)
                else:
                    DVE.op(DVE.eng.tensor_copy, pb, [stg], nowaw=(g > 0), out=stg.t[:, g * 512:(g + 1) * 512], in_=pap)
            self.qst.dma(self.y_out[b * BT + t * 128:b * BT + (t + 1) * 128, :], stg.t[:], reads=[stg], writes=[self.yB])

    def small(self):
        b = self.sm[self.sm_i]
        self.sm_i = (self.sm_i + 1) % len(self.sm)
        return b

    def ml_phase1(self, layer):
        j = layer // 2
        PE, ACT, DVE = self.PE, self.ACT, self.DVE
        self.take([self.mq8, self.mk8, self.mkt, self.mvs, self.gst])
        kscale = float(128.0 ** -0.5)
        gw32 = self.tmp32()
        self.qld.dma(gw32.t[:], self.mlwg_in[j], writes=[gw32])
        DVE.op(DVE.eng.tensor_copy, [gw32], [self.gw], out=self.gw.t[:], in_=gw32.t[:].rearrange("p (k c) -> p k c", c=32))
        for b in range(NB):
            tsl = slice(b * BT, (b + 1) * BT)
            self.load_x_block(layer, b)
            self.cast_x()
            for cq in range(13):
                s = self.load_slab('ml_w_in', j, 0, cq) if cq < 12 else self.gw
                if cq < 4:
                    for jj in range(4):
                        h = (cq % 2) * 4 + jj
                        pb, pap = self.bank()
                        self.mm_group(pap, [(s.t[:, kc, jj * 128:(jj + 1) * 128], self.xTb.t[:, kc, :]) for kc in range(KC)],
                                      [s, self.xTb], pb)
                        if cq < 2:
                            ACT.op(ACT.eng.copy, pb, [self.mq8], nowaw=True, out=self.mq8.t[:, h, :], in_=pap)
                        else:
                            ACT.op(ACT.eng.mul, pb, [self.mk8], nowaw=True, out=self.mk8.t[:, h, :], in_=pap, mul=kscale)
                if cq >= 2:
                    w = 32 if cq == 12 else 512
                    for t in range(4):
                        pb, pap = self.bank()
                        self.mm_group(pap[:, 0:w], [(self.xTb.t[:, kc, t * 128:(t + 1) * 128], s.t[:, kc, 0:w])
                                                    for kc in range(KC)], [s, self.xTb], pb)
                        if cq < 4:
                            DVE.op(DVE.eng.tensor_scalar, pb, [self.mkt], nowaw=True,
                                   out=self.mkt.t[:, t, (cq - 2) * 512:(cq - 1) * 512], in0=pap, scalar1=kscale,
                                   scalar2=None, op0=ALU.mult)
                        elif cq < 8:
                            DVE.op(DVE.eng.tensor_copy, pb, [self.mvs], nowaw=True,
                                   out=self.mvs.t[:, t, (cq - 4) * 512:(cq - 3) * 512], in_=pap)
                        elif cq < 12:
                            og = self.tmp32()
                            ACT.op(ACT.eng.activation, pb, [og], out=og.t[:], in_=pap, func=AF.Sigmoid)
                            self.qst.dma(self.o_ap[b * BT + t * 128:b * BT + (t + 1) * 128, (cq - 8) * 512:(cq - 7) * 512],
                                         og.t[:], reads=[og], writes=[self.oB[b]], nowaw=True)
                        else:
                            DVE.op(DVE.eng.tensor_tensor, pb + [self.mlb], [self.gst], nowaw=True, out=self.gst.t[:, t, :],
                                   in0=pap[:, 0:32], in1=self.mlb.t[:, j * 32:(j + 1) * 32], op=ALU.add)
            for f0 in (8, 24):
                tmp = self.small()
                tv = self.tmp32()
                ACT.op(ACT.eng.activation, [self.gst], [tv], out=tv.t[:, 0:32].rearrange("p (c g) -> p c g", g=8),
                       in_=self.gst.t[:, :, f0:f0 + 8], func=AF.Exp, scale=-1.0)
                ACT.op(ACT.eng.activation, [tv, self.epsc], [tv], out=tv.t[:, 0:32], in_=tv.t[:, 0:32], func=AF.Ln,
                       bias=self.epsc.t[:, 3:4])
                DVE.op(DVE.eng.tensor_scalar, [tv], [self.gst], nowaw=True, out=self.gst.t[:, :, f0:f0 + 8],
                       in0=tv.t[:, 0:32].rearrange("p (c g) -> p c g", g=8), scalar1=-1.0, scalar2=None, op0=ALU.mult)
            self.qst.dma(self.qT_ap[:, 0:8, tsl], self.mq8.t[:], reads=[self.mq8], writes=[self.qTB[b]])
            self.qst.dma(self.kT_ap[:, 0:8, tsl], self.mk8.t[:], reads=[self.mk8], writes=[self.kTB[b]])
            self.qst.dma(self.ktok_ap[tsl, :].rearrange("(c p) d -> p c d", p=128), self.mkt.t[:], reads=[self.mkt],
                         writes=[self.ktokB[b]])
            self.qst.dma(self.v_ap[tsl, :].rearrange("(c p) d -> p c d", p=128), self.mvs.t[:], reads=[self.mvs],
                         writes=[self.vB[b]])
            self.qst.dma(self.g_ap[b].rearrange("p (c g) -> p c g", g=32), self.gst.t[:], reads=[self.gst],
                         writes=[self.gB[b]])

    def ml_load_block(self, b):
        tsl = slice(b * BT, (b + 1) * BT)
        ld = self.qld.dma
        ld(self.qTc.t[:], self.qT_ap[:, 0:8, tsl], reads=[self.qTB[b]], writes=[self.qTc])
        ld(self.kTc.t[:], self.kT_ap[:, 0:8, tsl], reads=[self.kTB[b]], writes=[self.kTc])
        ld(self.ktk.t[:], self.ktok_ap[tsl, :].rearrange("(c p) d -> p c d", p=128), reads=[self.ktokB[b]], writes=[self.ktk])
        ld(self.gts.t[:], self.g_ap[b].rearrange("p (c g) -> p c g", g=32), reads=[self.gB[b]], writes=[self.gts])

    def ml_state_reset(self, zero):
        ACT, DVE = self.ACT, self.DVE
        if zero:
            DVE.op(DVE.eng.memset, [], [self.C32], self.C32.t[:], 0.0)
            DVE.op(DVE.eng.memset, [], [self.n32], self.n32.t[:], 0.0)
        else:
            DVE.op(DVE.eng.tensor_scalar, [self.C32, self.flags], [self.C32], out=self.C32.t[:], in0=self.C32.t[:],
                   scalar1=self.flags.t[:, 2:3], scalar2=None, op0=ALU.mult)
            DVE.op(DVE.eng.tensor_scalar, [self.n32, self.flags], [self.n32], out=self.n32.t[:], in0=self.n32.t[:],
                   scalar1=self.flags.t[:, 2:3], scalar2=None, op0=ALU.mult)
        ACT.op(ACT.eng.copy, [self.C32], [self.Cb], out=self.Cb.t[:], in_=self.C32.t[:])
        ACT.op(ACT.eng.copy, [self.n32], [self.nb], out=self.nb.t[:, :, 0], in_=self.n32.t[:])
        ACT.op(ACT.eng.copy, [self.n32], [self.nb], out=self.nb.t[:, :, 1], in_=self.n32.t[:])

    def ml_chunk(self, b, ch, dirn, hb):
        PE, ACT, DVE = self.PE, self.ACT, self.DVE
        fwd = dirn == 0
        gi, gf = (0, 8) if fwd else (16, 24)
        tri = self.cst.t[:, 2, :] if fwd else self.cst.t[:, 3, :]
        neg = self.cst.t[:, 4, :] if fwd else self.cst.t[:, 5, :]
        last = 127 if fwd else 0
        csl = slice(ch * 128, (ch + 1) * 128)
        va = self.vau[self.vau_i]
        self.vau_i ^= 1
        self.qld.dma(va.t[:], self.v_ap[b * BT + ch * 128:b * BT + (ch + 1) * 128, :], reads=[self.vB[b]], writes=[va])
        if self.stop <= 1:
            return
        lf = self.gts.t[:, ch, gf:gf + 8]
        li = self.gts.t[:, ch, gi:gi + 8]
        pbc, pbcap = self.bank()
        self.mm_group(pbcap[:, 0:32], [(tri, self.gts.t[:, ch, :])], [self.cst, self.gts], pbc)
        bias = self.small()
        DVE.op(DVE.eng.tensor_tensor, pbc + [self.gts], [bias], out=bias.t[:], in0=li, in1=pbcap[:, gf:gf + 8], op=ALU.subtract)
        if self.stop <= 2:
            return
        for hh in range(2):
            h0 = hh * 4
            Y = self.tmp32()
            for jx in range(4):
                DVE.op(DVE.eng.tensor_scalar, [self.cst, self.gts], [Y], nowaw=(jx > 0), out=Y.t[:, jx * 128:(jx + 1) * 128],
                       in0=tri, scalar1=self.gts.t[:, ch, gf + h0 + jx:gf + h0 + jx + 1], scalar2=None, op0=ALU.mult)
            px, pxap = self.bank()
            self.mm_group(pxap, [(self.onesf.t[:], Y.t[:])], [self.onesf, Y], px)
            if self.stop <= 3:
                return
            G = self.tmp32()
            ACT.op(ACT.eng.activation, px, [G], out=G.t[:], in_=pxap, func=AF.Exp)
            if self.stop <= 3.3:
                return
            Xm = self.tmp32()
            for jx in range(4):
                DVE.op(DVE.eng.tensor_tensor, px + [self.cst], [Xm], nowaw=(jx > 0),
                       out=Xm.t[:, jx * 128:(jx + 1) * 128], in0=pxap[:, jx * 128:(jx + 1) * 128], in1=neg, op=ALU.add)
                DVE.op(DVE.eng.tensor_scalar, [Xm, bias], [Xm], nowaw=True,
                       out=Xm.t[:, jx * 128:(jx + 1) * 128], in0=Xm.t[:, jx * 128:(jx + 1) * 128],
                       scalar1=bias.t[:, h0 + jx:h0 + jx + 1], scalar2=None, op0=ALU.add)
            if self.stop <= 3.6:
                return
            Dt = self.tmp32()
            ACT.op(ACT.eng.activation, [Xm], [Dt], out=Dt.t[:], in_=Xm.t[:], func=AF.Exp)
            if self.stop <= 4:
                return
            wsx = self.small()
            DVE.op(DVE.eng.tensor_tensor, px + [bias], [wsx], out=wsx.t[:, 0:4],
                   in0=pxap.rearrange("p (j l) -> p j l", l=128)[:, :, last], in1=bias.t[:, h0:h0 + 4], op=ALU.add)
            ws = self.small()
            ACT.op(ACT.eng.activation, [wsx], [ws], out=ws.t[:, 0:4], in_=wsx.t[:, 0:4], func=AF.Exp)
            if self.stop <= 5:
                return
            pS, pSap = self.bank()
            PE.acquire([self.kTc, self.qTc], pS)
            for jx in range(4):
                ins = PE.eng.matmul(out=pSap[:, jx * 128:(jx + 1) * 128], lhsT=self.kTc.t[:, h0 + jx, csl],
                                    rhs=self.qTc.t[:, h0 + jx, csl], start=True, stop=True)
            PE.release(ins, [self.kTc, self.qTc], pS)
            if self.stop <= 6:
                return
            AQ = self.tmpb()
            DVE.op(DVE.eng.tensor_tensor, pS + [Dt], [AQ], out=AQ.t[:, 0, :], in0=pSap, in1=Dt.t[:], op=ALU.mult)
            DVE.op(DVE.eng.tensor_tensor, [self.qTc, G], [AQ], nowaw=True,
                   out=AQ.t[:, 1, :].rearrange("p (j l) -> p j l", l=128), in0=self.qTc.t[:, h0:h0 + 4, csl],
                   in1=G.t[:].rearrange("p (j l) -> p j l", l=128), op=ALU.mult)
            if self.stop <= 7:
                return
            pN, pNap = self.bank(2)
            pD, pDap = self.bank()
            rd = [AQ, va, self.Cb, self.nb, self.onesb]
            PE.acquire(rd, pN + pD)
            for jx in range(4):
                h = h0 + jx
                nsl = pNap[:, jx // 2, (jx % 2) * 256:(jx % 2) * 256 + 256]
                PE.eng.matmul(out=nsl, lhsT=AQ.t[:, 0, jx * 128:(jx + 1) * 128], rhs=va.t[:, h * 256:(h + 1) * 256],
                              start=True, stop=False)
                PE.eng.matmul(out=nsl, lhsT=AQ.t[:, 1, jx * 128:(jx + 1) * 128], rhs=self.Cb.t[:, h, :],
                              start=False, stop=True)
                PE.eng.matmul(out=pDap[:, 2 * jx:2 * jx + 2], lhsT=AQ.t[:, 0, jx * 128:(jx + 1) * 128], rhs=self.onesb.t[:, 0:2],
                              start=True, stop=False)
                ins = PE.eng.matmul(out=pDap[:, 2 * jx:2 * jx + 2], lhsT=AQ.t[:, 1, jx * 128:(jx + 1) * 128],
                                    rhs=self.nb.t[:, h, :], start=False, stop=True)
            PE.release(ins, rd, pN + pD)
            if self.stop <= 8:
                return
            dn = self.small()
            ACT.op(ACT.eng.activation, pD, [dn], out=dn.t[:, 0:4], in_=pDap[:, 0:8].rearrange("p (j two) -> p j two", two=2)[:, :, 0],
                   func=AF.Abs)
            DVE.op(DVE.eng.tensor_scalar, [dn], [dn], out=dn.t[:, 0:4], in0=dn.t[:, 0:4], scalar1=1.0, scalar2=None, op0=ALU.max)
            DVE.op(DVE.eng.reciprocal, [dn], [dn], out=dn.t[:, 0:4], in_=dn.t[:, 0:4])
            for jx in range(4):
                h = h0 + jx
                nsl = pNap[:, jx // 2, (jx % 2) * 256:(jx % 2) * 256 + 256]
                ACT.op(ACT.eng.activation, pN + [dn], [hb], nowaw=(hh + jx > 0), out=hb.t[:, h * 256:(h + 1) * 256], in_=nsl,
                       func=AF.Identity, scale=dn.t[:, jx:jx + 1])
            if self.stop <= 9:
                return
            KW = self.tmpb()
            for jx in range(4):
                h = h0 + jx
                DVE.op(DVE.eng.tensor_scalar, [self.ktk, ws], [KW], nowaw=(jx > 0), out=KW.t[:, 0, jx * 128:(jx + 1) * 128],
                       in0=self.ktk.t[:, ch, h * 128:(h + 1) * 128], scalar1=ws.t[:, jx:jx + 1], scalar2=None, op0=ALU.mult)
            if self.stop <= 10:
                return
            pU, pUap = self.bank(2)
            pV, pVap = self.bank()
            rd = [KW, va, self.onesb]
            PE.acquire(rd, pU + pV)
            for jx in range(4):
                h = h0 + jx
                PE.eng.matmul(out=pUap[:, jx // 2, (jx % 2) * 256:(jx % 2) * 256 + 256], lhsT=KW.t[:, 0, jx * 128:(jx + 1) * 128],
                              rhs=va.t[:, h * 256:(h + 1) * 256], start=True, stop=True)
                ins = PE.eng.matmul(out=pVap[:, 2 * jx:2 * jx + 2], lhsT=KW.t[:, 0, jx * 128:(jx + 1) * 128], rhs=self.onesb.t[:, 0:2],
                                    start=True, stop=True)
            PE.release(ins, rd, pU + pV)
            if self.stop <= 11:
                return
            for jx in range(4):
                h = h0 + jx
                acol = G.t[:, jx * 128 + last:jx * 128 + last + 1]
                DVE.op(DVE.eng.scalar_tensor_tensor, pU + [self.C32, G], [self.C32], nowaw=True, out=self.C32.t[:, h, :],
                       in0=self.C32.t[:, h, :], scalar=acol, in1=pUap[:, jx // 2, (jx % 2) * 256:(jx % 2) * 256 + 256],
                       op0=ALU.mult, op1=ALU.add)
                DVE.op(DVE.eng.scalar_tensor_tensor, pV + [self.n32, G], [self.n32], nowaw=True, out=self.n32.t[:, h:h + 1],
                       in0=self.n32.t[:, h:h + 1], scalar=acol, in1=pVap[:, 2 * jx:2 * jx + 1], op0=ALU.mult, op1=ALU.add)
            if self.stop <= 12:
                return
            ACT.op(ACT.eng.copy, [self.C32], [self.Cb], nowaw=True, out=self.Cb.t[:, h0:h0 + 4, :], in_=self.C32.t[:, h0:h0 + 4, :])
            ACT.op(ACT.eng.copy, [self.n32], [self.nb], nowaw=True, out=self.nb.t[:, h0:h0 + 4, 0], in_=self.n32.t[:, h0:h0 + 4])
            ACT.op(ACT.eng.copy, [self.n32], [self.nb], nowaw=True, out=self.nb.t[:, h0:h0 + 4, 1], in_=self.n32.t[:, h0:h0 + 4])

    def ml_fwd(self, layer):
        self.take([self.qTc, self.kTc, self.ktk, self.vau[0], self.vau[1], self.gts, self.hbuf[0], self.hbuf[1], self.obuf])
        self.vau_i = 0
        self.ml_state_reset(True)
        hi = 0
        for b in range(NB):
            self.ml_load_block(b)
            for ch in range(4):
                if b == 4 and ch == 0:
                    self.ml_state_reset(False)
                hb = self.hbuf[hi]
                hi ^= 1
                self.ml_chunk(b, ch, 0, hb)
                self.qst.dma(self.hf_ap[b * BT + ch * 128:b * BT + (ch + 1) * 128, :], hb.t[:], reads=[hb],
                             writes=[self.hfB[b]], nowaw=True)

    def ml_bwd_block(self, layer, b):
        PE, ACT, DVE = self.PE, self.ACT, self.DVE
        j = layer // 2
        self.take([self.qTc, self.kTc, self.ktk, self.vau[0], self.vau[1], self.gts, self.hbuf[0], self.hbuf[1], self.obuf])
        if b == NB - 1:
            self.ml_state_reset(True)
        self.ml_load_block(b)
        hb, hf = self.hbuf
        for ch in range(3, -1, -1):
            if b == 3 and ch == 3:
                self.ml_state_reset(False)
            rows = slice(b * BT + ch * 128, b * BT + (ch + 1) * 128)
            self.qld.dma(hf.t[:], self.hf_ap[rows, :], reads=[self.hfB[b]], writes=[hf])
            self.qld.dma(self.obuf.t[:], self.o_ap[rows, :], reads=[self.oB[b]], writes=[self.obuf])
            self.ml_chunk(b, ch, 1, hb)
            DVE.op(DVE.eng.tensor_tensor, [hb, hf], [hb], out=hb.t[:], in0=hb.t[:], in1=hf.t[:], op=ALU.add)
            DVE.op(DVE.eng.tensor_tensor, [hb], [hf], out=hf.t[:], in0=hb.t[:], in1=hb.t[:], op=ALU.mult)
            ss = self.small()
            DVE.op(DVE.eng.tensor_reduce, [hf], [ss], out=ss.t[:], in_=hf.t[:].rearrange("p (h d) -> p h d", d=256),
                   axis=AX.X, op=ALU.add)
            ACT.op(ACT.eng.activation, [ss, self.epsc], [ss], out=ss.t[:], in_=ss.t[:], func=AF.Sqrt, bias=self.epsc.t[:, 2:3],
                   scale=1.0 / 256.0)
            DVE.op(DVE.eng.reciprocal, [ss], [ss], out=ss.t[:], in_=ss.t[:])
            for h in range(8):
                ACT.op(ACT.eng.activation, [hb, ss], [hb], out=hb.t[:, h * 256:(h + 1) * 256], in_=hb.t[:, h * 256:(h + 1) * 256],
                       func=AF.Identity, scale=ss.t[:, h:h + 1])
            DVE.op(DVE.eng.tensor_tensor, [hb, self.obuf], [hb], out=hb.t[:], in0=hb.t[:], in1=self.obuf.t[:], op=ALU.mult)
            for g in range(4):
                pb, pap = self.bank()
                PE.acquire([hb, self.cst], pb)
                for i in range(4):
                    c = g * 4 + i
                    ins = PE.eng.transpose(out=pap[:, i * 128:(i + 1) * 128], in_=hb.t[:, c * 128:(c + 1) * 128],
                                           identity=self.cst.t[:, 0, :])
                PE.release(ins, [hb, self.cst], pb)
                for i in range(4):
                    c = g * 4 + i
                    gcol = self.mlg.t[:, j * KC + c:j * KC + c + 1]
                    if i % 2 == 0:
                        ACT.op(ACT.eng.activation, pb + [self.mlg], [self.OT], nowaw=True, out=self.OT.t[:, c, ch * 128:(ch + 1) * 128],
                               in_=pap[:, i * 128:(i + 1) * 128], func=AF.Identity, scale=gcol)
                    else:
                        DVE.op(DVE.eng.tensor_scalar, pb + [self.mlg], [self.OT], nowaw=True,
                               out=self.OT.t[:, c, ch * 128:(ch + 1) * 128], in0=pap[:, i * 128:(i + 1) * 128], scalar1=gcol,
                               scalar2=None, op0=ALU.mult)
        self.load_x_block(layer, b)


def _const_tables():
    p = np.arange(128)
    ident = np.eye(128, dtype=np.float32)
    i64 = p % 64
    partner = np.where(i64 < 32, p + 32, p - 32)
    perm = np.zeros((128, 128), np.float32)
    perm[partner, p] = 1.0
    triU = (p[:, None] <= p[None, :]).astype(np.float32)
    triL = (p[:, None] >= p[None, :]).astype(np.float32)
    negF = np.where(p[:, None] > p[None, :], NEG, 0.0).astype(np.float32)
    negB = np.where(p[:, None] < p[None, :], NEG, 0.0).astype(np.float32)
    return np.ascontiguousarray(np.stack([ident, perm, triU, triL, negF, negB], axis=1))


def _rope_tables(seq_len):
    t = np.arange(NTOK) % seq_len
    row = (t // 64).astype(np.float32)
    col = (t % 64).astype(np.float32)
    inv_freq = (np.float32(10000.0) ** (-np.arange(0, 64, 2, dtype=np.float32) / np.float32(64))).astype(np.float32)
    C = np.zeros((128, NTOK), np.float32)
    S = np.zeros((128, NTOK), np.float32)
    for m in range(128):
        sec, i = m // 64, m % 64
        f = i % 32
        ang = ((row if sec == 0 else col) * inv_freq[f]).astype(np.float32)
        C[m] = np.cos(ang)
        S[m] = -np.sin(ang) if i < 32 else np.sin(ang)
    return np.stack([C, S], axis=0).astype(np.float32)


_NC_CACHE = {}


def _get_nc(nlayers=DEPTH, debug=False):
    key = (nlayers, debug)
    if key not in _NC_CACHE:
        k = K(nlayers, debug)
        k.build()
        _NC_CACHE[key] = k
    return _NC_CACHE[key]


def _prep_inputs(inp, nlayers=DEPTH):
    f = lambda a: np.ascontiguousarray(np.asarray(a, dtype=np.float32))
    xp, xs = f(inp['x_prompt']), f(inp['x_sample'])
    mp, ms = f(inp['mem_prompt']), f(inp['mem_sample'])
    cst = _const_tables()
    rope_p = _rope_tables(2048)
    rope_s = _rope_tables(4096)
    lng = f(inp['ln_g']).reshape(12, KC, 128).transpose(2, 0, 1).reshape(128, 12 * KC)
    lnb = f(inp['ln_b']).reshape(12, KC, 128).transpose(2, 0, 1).reshape(128, 12 * KC)
    qkg = np.stack([f(inp['attn_q_gain'])[0], f(inp['attn_q_gain'])[1],
                    f(inp['attn_k_gain'])[0], f(inp['attn_k_gain'])[1]], axis=1)
    mlb = np.broadcast_to(f(inp['ml_b_gate']).reshape(1, 64), (128, 64))
    mlg = f(inp['ml_head_gain']).reshape(2, KC, 128).transpose(2, 0, 1).reshape(128, 2 * KC)
    shared = {nm: f(inp[nm]) for nm in W_SHAPES}
    shared['mlwg'] = np.ascontiguousarray(
        f(inp['ml_w_in'])[:, :, 6144:6176].reshape(2, KC, 128, 32).transpose(0, 2, 1, 3).reshape(2, 128, KC * 32))
    if nlayers != DEPTH:
        for nm in W_SHAPES:
            lead = max(1, ((nlayers + 1) // 2 if nm.startswith('attn') else (nlayers // 2 if nm.startswith('ml_') else nlayers)))
            shared[nm] = np.ascontiguousarray(shared[nm][:lead])
    shared.update(cst=cst, lng=np.ascontiguousarray(lng), lnb=np.ascontiguousarray(lnb),
                  qkg=np.ascontiguousarray(qkg), mlb=np.ascontiguousarray(mlb), mlg=np.ascontiguousarray(mlg))
    in_maps = []
    for c in range(8):
        m = dict(shared)
        fl = np.zeros((128, 4), np.float32)
        if c < 4:
            m['x'] = np.ascontiguousarray(xp[2 * c:2 * c + 2].reshape(NTOK, D))
            m['mem'] = np.ascontiguousarray(mp[2 * c:2 * c + 2].reshape(512, D))
            m['rope'] = rope_p
            fl[:, 1] = NEG
            fl[:, 2] = 0.0
        else:
            m['x'] = np.ascontiguousarray(xs[c - 4])
            m['mem'] = np.ascontiguousarray(np.concatenate([ms[c - 4], ms[c - 4]], axis=0))
            m['rope'] = rope_s
            fl[:, 1] = 0.0
            fl[:, 2] = 1.0
        m['flags'] = fl
        in_maps.append(m)
    return in_maps


def kernel(**inputs):
    k = _get_nc()
    in_maps = _prep_inputs(inputs)
    res = run_bass_kernel_spmd(k.nc, in_maps, core_ids=list(range(8)))
    ys = [np.asarray(r['y'], dtype=np.float32) for r in res.results]
    y_prompt = np.stack([ys[c].reshape(2, 2048, D) for c in range(4)], axis=0).reshape(8, 2048, D)
    y_sample = np.stack([ys[c] for c in range(4, 8)], axis=0)
    return (y_prompt, y_sample)
```
